# Optimizing a Trainium2 kernel written in Bass

```python
import math
import jax, jax.numpy as jnp
from jax import lax
import numpy as np

D_MODEL = 1024
BATCH = 8
SEQ = 8192
DEPTH = 4

POOL_WINDOWS = (2, 4, 8, 16)
N_POOL_GROUPS = len(POOL_WINDOWS)
POOL_GROUP_DIM = D_MODEL // 8
POOL_W = N_POOL_GROUPS * POOL_GROUP_DIM
FOX_HEADS = 8
FOX_HEAD_DIM = 64
FOX_W = FOX_HEADS * FOX_HEAD_DIM
Q_BLOCK = 128
OFF_POOL = 0
OFF_Q = OFF_POOL + POOL_W
OFF_K = OFF_Q + FOX_W
OFF_V = OFF_K + FOX_W
OFF_F = OFF_V + FOX_W
OFF_GP = OFF_F + FOX_HEADS
OFF_GF = OFF_GP + D_MODEL
IN_W = OFF_GF + D_MODEL
N_MEM = 256
X_HEADS = 4
X_HEAD_DIM = 128
X_W = X_HEADS * X_HEAD_DIM
D_FF = 2816
CONV_WIDTH = 3
RMS_EPS = 1e-6

kernel_name = "hybrid_pool_fox_memxattn_convffn"


def rms_norm(x, g):
    xf = x.astype(jnp.float32)
    y = xf * lax.rsqrt(jnp.mean(xf * xf, axis=-1, keepdims=True) + RMS_EPS)
    return (y * g.astype(jnp.float32)).astype(x.dtype)


def shift_right(z, n):
    return jnp.pad(z, ((0, 0), (n, 0), (0, 0)))[:, : z.shape[1]]


def pool_mixer(u, pool_w, pool_scale):
    B, S, _ = u.shape
    uf = u.astype(jnp.float32)
    cs = jnp.cumsum(uf, axis=1)
    t = jnp.arange(S)
    outs = []
    for g, w in enumerate(POOL_WINDOWS):
        sl = slice(g * POOL_GROUP_DIM, (g + 1) * POOL_GROUP_DIM)
        csg = cs[..., sl]
        cnt = jnp.minimum(t + 1, w).astype(jnp.float32)[None, :, None]
        outs.append((csg - shift_right(csg, w)) / cnt - uf[..., sl])
    pooled = jnp.stack(outs, axis=2).astype(u.dtype)
    mixed = jnp.einsum("bsgc,gcd->bsgd", pooled, pool_w).reshape(B, S, POOL_W)
    return mixed * pool_scale


def fox_attention(q, k, v, log_f):
    B, S, H, Dh = q.shape
    nb = S // Q_BLOCK
    scale = 1.0 / math.sqrt(Dh)
    cT = jnp.cumsum(log_f.astype(jnp.float32), axis=1).transpose(0, 2, 1)
    qb = q.reshape(B, nb, Q_BLOCK, H, Dh).transpose(1, 0, 2, 3, 4)
    cb = cT.reshape(B, H, nb, Q_BLOCK).transpose(2, 0, 1, 3)
    kpos = jnp.arange(S)

    def one_block(args):
        qi, ci, bi = args
        s = jnp.einsum("bqhd,bkhd->bhqk", qi, k, preferred_element_type=jnp.float32) * scale
        s = s + ci[..., :, None] - cT[:, :, None, :]
        qpos = bi * Q_BLOCK + jnp.arange(Q_BLOCK)
        s = jnp.where(kpos[None, :] <= qpos[:, None], s, -jnp.inf)
        p = jax.nn.softmax(s, axis=-1)
        return jnp.einsum("bhqk,bkhd->bqhd", p.astype(v.dtype), v)

    out = lax.map(one_block, (qb, cb, jnp.arange(nb)))
    return out.transpose(1, 0, 2, 3, 4).reshape(B, S, H * Dh)


def mem_attention(h, mem_n, w_xq, w_xkv, w_xo):
    B, S, _ = h.shape
    M = mem_n.shape[1]
    q = (h @ w_xq).reshape(B, S, X_HEADS, X_HEAD_DIM)
    kv = mem_n @ w_xkv
    k = kv[..., :X_W].reshape(B, M, X_HEADS, X_HEAD_DIM)
    v = kv[..., X_W:].reshape(B, M, X_HEADS, X_HEAD_DIM)
    s = jnp.einsum("bqhd,bmhd->bhqm", q, k, preferred_element_type=jnp.float32) / math.sqrt(X_HEAD_DIM)
    p = jax.nn.softmax(s, axis=-1)
    o = jnp.einsum("bhqm,bmhd->bqhd", p.astype(v.dtype), v).reshape(B, S, X_W)
    return o @ w_xo


def conv_ffn(h, w_up, conv_w, conv_b, w_down):
    z = h @ w_up
    zc = conv_w[2] * z + conv_w[1] * shift_right(z, 1) + conv_w[0] * shift_right(z, 2) + conv_b
    g, u = zc[..., :D_FF], zc[..., D_FF:]
    return (jax.nn.gelu(g, approximate=True) * u) @ w_down


def setup_inputs(seed: int = 0) -> dict:
    key = jax.random.key(seed)
    ks = jax.random.split(key, 32)
    L, D = DEPTH, D_MODEL
    f32 = jnp.float32

    def nrm(k, shape, fan_in):
        return jax.random.normal(k, shape, f32) * (fan_in ** -0.5)

    def gain(k):
        return 1.0 + 0.05 * jax.random.normal(k, (L, D), f32)

    b_forget = (jnp.linspace(1.0, 5.0, FOX_HEADS, dtype=f32)[None, :]
                + 0.1 * jax.random.normal(ks[5], (L, FOX_HEADS), f32))
    return {
        "x": jax.random.normal(ks[0], (BATCH, SEQ, D), f32),
        "mem": jax.random.normal(ks[1], (BATCH, N_MEM, D), f32),
        "mix_pre_g": gain(ks[2]),
        "mix_post_g": gain(ks[3]),
        "w_in": nrm(ks[4], (L, D, IN_W), D),
        "b_forget": b_forget,
        "pool_w": nrm(ks[6], (L, N_POOL_GROUPS, POOL_GROUP_DIM, POOL_GROUP_DIM), POOL_GROUP_DIM),
        "pool_scale": 1.0 + 0.1 * jax.random.normal(ks[7], (L, POOL_W), f32),
        "w_pool_br": nrm(ks[8], (L, POOL_W, D), POOL_W),
        "w_fox_br": nrm(ks[9], (L, FOX_W, D), FOX_W),
        "w_mix_out": nrm(ks[10], (L, D, D), D),
        "xa_pre_g": gain(ks[11]),
        "xa_post_g": gain(ks[12]),
        "mem_g": gain(ks[13]),
        "w_xq": nrm(ks[14], (L, D, X_W), D),
        "w_xkv": nrm(ks[15], (L, D, 2 * X_W), D),
        "w_xo": nrm(ks[16], (L, X_W, D), X_W),
        "ffn_pre_g": gain(ks[17]),
        "ffn_post_g": gain(ks[18]),
        "w_up": nrm(ks[19], (L, D, 2 * D_FF), D),
        "conv_w": nrm(ks[20], (L, CONV_WIDTH, 2 * D_FF), CONV_WIDTH),
        "conv_b": 0.01 * jax.random.normal(ks[21], (L, 2 * D_FF), f32),
        "w_down": nrm(ks[22], (L, D_FF, D), D_FF),
    }


def reference(x, mem, mix_pre_g, mix_post_g, w_in, b_forget, pool_w, pool_scale,
              w_pool_br, w_fox_br, w_mix_out, xa_pre_g, xa_post_g, mem_g, w_xq, w_xkv,
              w_xo, ffn_pre_g, ffn_post_g, w_up, conv_w, conv_b, w_down):
    B, S, _ = x.shape
    for l in range(DEPTH):
        h = rms_norm(x, mix_pre_g[l])
        z = h @ w_in[l]
        u_pool = z[..., OFF_POOL:OFF_Q]
        q = z[..., OFF_Q:OFF_K].reshape(B, S, FOX_HEADS, FOX_HEAD_DIM)
        k = z[..., OFF_K:OFF_V].reshape(B, S, FOX_HEADS, FOX_HEAD_DIM)
        v = z[..., OFF_V:OFF_F].reshape(B, S, FOX_HEADS, FOX_HEAD_DIM)
        log_f = jax.nn.log_sigmoid((z[..., OFF_F:OFF_GP] + b_forget[l]).astype(jnp.float32))
        gate_pool = jax.nn.sigmoid(z[..., OFF_GP:OFF_GF])
        gate_fox = jax.nn.sigmoid(z[..., OFF_GF:])
        y_pool = pool_mixer(u_pool, pool_w[l], pool_scale[l]) @ w_pool_br[l]
        y_fox = fox_attention(q, k, v, log_f) @ w_fox_br[l]
        merged = gate_pool * y_pool + gate_fox * y_fox
        x = x + rms_norm(merged @ w_mix_out[l], mix_post_g[l])
        h = rms_norm(x, xa_pre_g[l])
        mem_n = rms_norm(mem, mem_g[l])
        x = x + rms_norm(mem_attention(h, mem_n, w_xq[l], w_xkv[l], w_xo[l]), xa_post_g[l])
        h = rms_norm(x, ffn_pre_g[l])
        x = x + rms_norm(conv_ffn(h, w_up[l], conv_w[l], conv_b[l], w_down[l]), ffn_post_g[l])
    return x
```

```python
import contextlib
import math
import numpy as np
import ml_dtypes
import concourse.bass as bass
import concourse.mybir as mybir
from concourse.bass_utils import run_bass_kernel_spmd

F32 = mybir.dt.float32
BF16 = mybir.dt.bfloat16
AF = mybir.ActivationFunctionType
ALU = mybir.AluOpType

ENGS = ("pe", "act", "dve", "pool", "sp")
D = 1024
NH = 8
DFF = 2816
IN_W = 4104
OFF_Q, OFF_K, OFF_V, OFF_F, OFF_GP, OFF_GF = 512, 1024, 1536, 2048, 2056, 3080
NMEM = 256
EPS = 1e-6
NEG = -30000.0


class Buf:
    __slots__ = ("name", "lastw", "reads", "dsem", "dcount")

    def __init__(self, name):
        self.name = name
        self.lastw = None
        self.reads = {}
        self.dsem = None
        self.dcount = 0


class Sched:
    def __init__(self, nc, es):
        self.nc = nc
        self.es = es
        self.streams = {e: [] for e in ENGS}
        self.nops = {e: 0 for e in ENGS}
        self.waited = {e: {} for e in ENGS}
        self.latest = {}
        self.sems = {}
        self.free_dsems = []
        self.phase_bufs = []
        self.ndsem = 0

    def sem(self, k):
        if k not in self.sems:
            self.sems[k] = self.es.enter_context(self.nc.semaphore(k))
        return self.sems[k]

    def _dsem(self, buf):
        if buf.dsem is None:
            if self.free_dsems:
                buf.dsem, buf.dcount = self.free_dsems.pop()
            else:
                buf.dsem = "D_%d" % self.ndsem
                self.ndsem += 1
            if not buf.name.startswith("P_"):
                self.phase_bufs.append(buf)
        return buf.dsem

    def _deps(self, eng, reads, writes):
        own = "E_" + eng
        need = {}

        def add(tok, raw):
            k, v = tok
            if k == own and (eng in ("pe", "sp") or not raw):
                return
            if v > need.get(k, 0):
                need[k] = v

        for b in reads:
            if b.lastw is not None:
                add(b.lastw, True)
        for b in writes:
            if b.lastw is not None:
                add(b.lastw, False)
            for k, v in b.reads.items():
                add((k, v), False)
        out = []
        w = self.waited[eng]
        for k, v in need.items():
            if w.get(k, 0) >= v:
                continue
            w[k] = v
            out.append((k, v))
        return out

    def _commit(self, tok, reads, writes):
        k, v = tok
        self.latest[k] = max(self.latest.get(k, 0), v)
        for b in reads:
            if b.reads.get(k, 0) < v:
                b.reads[k] = v
        for b in writes:
            b.lastw = tok
            b.reads = {}

    def op(self, eng, fn, reads=(), writes=()):
        waits = self._deps(eng, reads, writes)
        self.nops[eng] += 1
        tok = ("E_" + eng, self.nops[eng])
        self.streams[eng].append((waits, fn, (tok[0], 1)))
        self._commit(tok, reads, writes)

    def dma(self, eng, fn, reads=(), writes=(), sembuf=None):
        waits = self._deps(eng, reads, writes)
        k = self._dsem(sembuf)
        sembuf.dcount += 16
        tok = (k, sembuf.dcount)
        self.streams[eng].append((waits, fn, (k, 16)))
        self._commit(tok, reads, writes)

    def barrier(self):
        for e in ENGS:
            w = self.waited[e]
            waits = []
            for k, v in self.latest.items():
                if w.get(k, 0) < v:
                    w[k] = v
                    waits.append((k, v))
            if waits:
                self.streams[e].append((waits, None, None))

    def flush(self):
        self.barrier()
        nc = self.nc
        engobj = {"pe": "tensor", "act": "scalar", "dve": "vector", "pool": "gpsimd", "sp": "sync"}
        for e in ENGS:
            for waits, fn, inc in self.streams[e]:
                for (k, v) in waits:
                    self.sem(k)
                if inc is not None:
                    self.sem(inc[0])
        with nc.Block() as block:
            def make(e):
                stream = self.streams[e]

                def body(eobj):
                    for waits, fn, inc in stream:
                        for (k, v) in waits:
                            eobj.wait_ge(self.sems[k], v)
                        if fn is not None:
                            fn(eobj).then_inc(self.sems[inc[0]], inc[1])
                return body

            for e in ENGS:
                if self.streams[e]:
                    getattr(block, engobj[e])(make(e))
        self.ninstr = getattr(self, "ninstr", 0) + sum(len(s) for s in self.streams.values())
        self.streams = {e: [] for e in ENGS}
        for b in self.phase_bufs:
            self.free_dsems.append((b.dsem, b.dcount))
            b.dsem = None
        self.phase_bufs = []


def build_program(SEQ, NL, dbg=()):
    nc = bass.Bass("TRN2", target_bir_lowering=False)
    NB = SEQ // 512
    NT = SEQ // 128

    def din(name, shape, dt=F32):
        return nc.dram_tensor(name, list(shape), dt, kind="ExternalInput").ap()

    def dscr(name, shape, dt):
        kind = "ExternalOutput" if name in dbg else "Internal"
        return nc.dram_tensor(name, list(shape), dt, kind=kind).ap()

    x_in = din("x", [SEQ, D])
    mem_in = din("mem", [NMEM, D])
    gains_in = {n: din(n, [NL, D]) for n in
                ("mix_pre_g", "mix_post_g", "xa_pre_g", "xa_post_g", "mem_g", "ffn_pre_g", "ffn_post_g")}
    w_in_d = din("w_in", [NL, D, IN_W])
    b_forget_d = din("b_forget", [NL, NH])
    pool_w_d = din("pool_w", [NL, 4, 128, 128])
    pool_scale_d = din("pool_scale", [NL, 512])
    w_pool_br_d = din("w_pool_br", [NL, 512, D])
    w_fox_br_d = din("w_fox_br", [NL, 512, D])
    w_mix_out_d = din("w_mix_out", [NL, D, D])
    w_xq_d = din("w_xq", [NL, D, 512])
    w_xkv_d = din("w_xkv", [NL, D, 1024])
    w_xo_d = din("w_xo", [NL, 512, D])
    w_up_d = din("w_up", [NL, D, 2 * DFF])
    conv_w_d = din("conv_w", [NL, 3, 2 * DFF])
    conv_b_d = din("conv_b", [NL, 2 * DFF])
    w_down_d = din("w_down", [NL, DFF, D])
    identf_d = din("c_identf", [128, 128])
    e0_d = din("c_e0", [128, 128])
    invc_d = din("c_invc", [128, 64])
    identb_d = din("c_identb", [128, 128], BF16)
    onesb_d = din("c_onesb", [128, 128], BF16)
    maskneg_d = din("c_maskneg", [128, 128], BF16)
    out_d = nc.dram_tensor("out", [SEQ, D], F32, kind="ExternalOutput").ap()

    xT = dscr("s_xT", [D, SEQ], F32)
    wb = {
        "w_in": dscr("s_w_in", [NL, D, IN_W], BF16),
        "pool_w": dscr("s_pool_w", [NL, 512, 128], BF16),
        "w_pool_br": dscr("s_w_pool_br", [NL, 512, D], BF16),
        "w_fox_br": dscr("s_w_fox_br", [NL, 512, D], BF16),
        "w_mix_out": dscr("s_w_mix_out", [NL, D, D], BF16),
        "w_xq": dscr("s_w_xq", [NL, D, 512], BF16),
        "w_xkv": dscr("s_w_xkv", [NL, D, 1024], BF16),
        "w_xo": dscr("s_w_xo", [NL, 512, D], BF16),
        "w_up": dscr("s_w_up", [NL, D, 2 * DFF], BF16),
        "w_down": dscr("s_w_down", [NL, DFF, D], BF16),
    }
    wsrc = {"w_in": w_in_d, "pool_w": pool_w_d.rearrange("l g c d -> l (g c) d"), "w_pool_br": w_pool_br_d,
            "w_fox_br": w_fox_br_d, "w_mix_out": w_mix_out_d, "w_xq": w_xq_d, "w_xkv": w_xkv_d,
            "w_xo": w_xo_d, "w_up": w_up_d, "w_down": w_down_d}
    QT = dscr("s_QT", [512, SEQ], BF16)
    KT = dscr("s_KT", [512, SEQ], BF16)
    Vs = dscr("s_Vs", [128, NH, NT, 64], BF16)
    Cs = dscr("s_Cs", [NH, SEQ], F32)
    CT = dscr("s_CT", [NH, 2, SEQ], BF16)
    Asc = dscr("s_A", [D, SEQ], BF16)
    Gfsc = dscr("s_Gf", [D, SEQ], BF16)
    OT = dscr("s_OT", [512, SEQ], BF16)
    GU = dscr("s_GU", [DFF, SEQ], BF16)

    Bdram = Buf("P_dram")
    es = contextlib.ExitStack()
    S = Sched(nc, es)

    _uid = [0]

    def sb(st, name, shape, dt):
        _uid[0] += 1
        return st.enter_context(nc.sbuf_tensor("%s_u%d" % (name, _uid[0]), list(shape), dt))

    with es:
        identf = sb(es, "identf", [128, 128], F32)
        e0 = sb(es, "e0", [128, 128], F32)
        invc = sb(es, "invc", [128, 64], F32)
        identb = sb(es, "identb", [128, 128], BF16)
        onesb = sb(es, "onesb", [128, 128], BF16)
        maskneg = sb(es, "maskneg", [128, 128], BF16)
        gains = sb(es, "gains", [128, 7, NL * 8], F32)
        pscale = sb(es, "pscale", [128, NL * 4], F32)
        convw = sb(es, "convw", [128, NL * 3 * 44], F32)
        convb = sb(es, "convb", [128, NL * 44], F32)
        negbf = sb(es, "negbf", [8, NL], F32)
        memT = sb(es, "memT", [128, 8, NMEM], F32)
        ones8 = sb(es, "ones8", [8, 512], F32)
        Bconst = Buf("P_const")
        Bmem = Buf("P_memT")
        psum = [es.enter_context(nc.psum_tensor("ps%d" % i, [128, 512], F32)) for i in range(8)]
        Bps = [Buf("ps%d" % i) for i in range(8)]
        GI = {n: i for i, n in enumerate(
            ("mix_pre_g", "mix_post_g", "xa_pre_g", "xa_post_g", "mem_g", "ffn_pre_g", "ffn_post_g"))}

        def gcol(name, l, c):
            return gains[:, GI[name], l * 8 + c:l * 8 + c + 1]

        class Rot:
            def __init__(self, ids):
                self.ids = list(ids)
                self.i = 0

            def next(self):
                v = self.ids[self.i % len(self.ids)]
                self.i += 1
                return v

        Bcast = Buf("P_wcast")

        def cast_weights(l):
            for name in ("w_in", "pool_w", "w_pool_br", "w_fox_br", "w_mix_out", "w_xq", "w_xkv", "w_xo",
                         "w_up", "w_down"):
                src = wsrc[name][l]
                dst = wb[name][l]
                rows = src.shape[0]
                for r0 in range(0, rows, 128):
                    S.dma("pool", lambda e, s=src[r0:r0 + 128, :], d=dst[r0:r0 + 128, :]: e.dma_start(out=d, in_=s, max_dma_last_dim=4096),
                          reads=[], writes=[Bdram], sembuf=Bcast)

        def load_w(dst_tile, name, l, Bw, kch):
            S.dma("sp", lambda e: e.dma_start(out=dst_tile[:], in_=wb[name][l].rearrange("(k p) n -> p k n", p=128)),
                  reads=[Bdram], writes=[Bw], sembuf=Bw)

        with contextlib.ExitStack() as st:
            for t, d_ in ((identf, identf_d), (e0, e0_d), (invc, invc_d), (identb, identb_d), (onesb, onesb_d),
                          (maskneg, maskneg_d)):
                S.dma("sp", lambda e, t=t, d_=d_: e.dma_start(out=t[:], in_=d_), writes=[Bconst], sembuf=Bconst)
            S.op("dve", lambda e: e.memset(ones8[:], 1.0), writes=[Bconst])
            stg = sb(st, "pstg", [128, 128], F32)
            Bstg = Buf("pstg")
            rot = Rot(range(8))

            def load_rows_T(dst, src2d, width):
                nrows = src2d.shape[0]
                for r0 in range(0, nrows, 128):
                    n = min(128, nrows - r0)
                    S.dma("sp", lambda e, r0=r0, n=n: e.dma_start(out=stg[0:n, 0:width], in_=src2d[r0:r0 + n, :]),
                          writes=[Bstg], sembuf=Bstg)
                    b = rot.next()
                    S.op("pe", lambda e, n=n, b=b: e.transpose(psum[b][0:width, 0:n], stg[0:n, 0:width],
                                                              identf[0:n, 0:n]),
                         reads=[Bstg, Bconst], writes=[Bps[b]])
                    S.op("dve", lambda e, r0=r0, n=n, b=b: e.tensor_copy(out=dst[0:width, r0:r0 + n],
                                                                        in_=psum[b][0:width, 0:n]),
                         reads=[Bps[b]], writes=[Bconst])

            for n_, i_ in GI.items():
                load_rows_T(gains[:, i_, :], gains_in[n_].rearrange("l (c p) -> (l c) p", p=128), 128)
            load_rows_T(pscale, pool_scale_d.rearrange("l (c p) -> (l c) p", p=128), 128)
            load_rows_T(convw, conv_w_d.rearrange("l j (c p) -> (l j c) p", p=128), 128)
            load_rows_T(convb, conv_b_d.rearrange("l (c p) -> (l c) p", p=128), 128)
            load_rows_T(negbf, b_forget_d, 8)
            S.op("dve", lambda e: e.tensor_scalar(out=negbf[:], in0=negbf[:], scalar1=-1.0, scalar2=None,
                                                  op0=ALU.mult), reads=[Bconst], writes=[Bconst])
            mstg = sb(st, "mstg", [128, D], F32)
            Bmstg = Buf("mstg")
            for mt in range(NMEM // 128):
                S.dma("sp", lambda e, mt=mt: e.dma_start(out=mstg[:], in_=mem_in[mt * 128:(mt + 1) * 128, :]),
                      writes=[Bmstg], sembuf=Bmstg)
                for c in range(8):
                    b = rot.next()
                    S.op("pe", lambda e, c=c, b=b: e.transpose(psum[b][:, 0:128], mstg[:, c * 128:(c + 1) * 128],
                                                              identf[:]),
                         reads=[Bmstg, Bconst], writes=[Bps[b]])
                    S.op("dve", lambda e, c=c, b=b, mt=mt: e.tensor_copy(out=memT[:, c, mt * 128:(mt + 1) * 128],
                                                                        in_=psum[b][:, 0:128]),
                         reads=[Bps[b]], writes=[Bmem])
            cast_weights(0)
            xin = [sb(st, "xin%d" % i, [128, D], F32) for i in range(2)]
            Bxin = [Buf("xin%d" % i) for i in range(2)]
            xo = [sb(st, "xo%d" % i, [128, 8, 512], F32) for i in range(2)]
            Bxo = [Buf("xo%d" % i) for i in range(2)]
            for T in range(NB):
                so = T % 2
                for i4 in range(4):
                    tt = T * 4 + i4
                    si = tt % 2
                    S.dma("sp", lambda e, tt=tt, si=si: e.dma_start(out=xin[si][:], in_=x_in[tt * 128:(tt + 1) * 128, :]),
                          writes=[Bxin[si]], sembuf=Bxin[si])
                    for c in range(8):
                        b = rot.next()
                        S.op("pe", lambda e, c=c, b=b, si=si: e.transpose(psum[b][:, 0:128],
                                                                         xin[si][:, c * 128:(c + 1) * 128], identf[:]),
                             reads=[Bxin[si], Bconst], writes=[Bps[b]])
                        eng = "dve" if c % 2 == 0 else "act"
                        if eng == "dve":
                            S.op("dve", lambda e, c=c, b=b, so=so, i4=i4: e.tensor_copy(
                                out=xo[so][:, c, i4 * 128:(i4 + 1) * 128], in_=psum[b][:, 0:128]),
                                reads=[Bps[b]], writes=[Bxo[so]])
                        else:
                            S.op("act", lambda e, c=c, b=b, so=so, i4=i4: e.activation(
                                out=xo[so][:, c, i4 * 128:(i4 + 1) * 128], in_=psum[b][:, 0:128], func=AF.Copy),
                                reads=[Bps[b]], writes=[Bxo[so]])
                S.dma("sp", lambda e, T=T, so=so: e.dma_start(
                    out=xT[:, T * 512:(T + 1) * 512].rearrange("(c p) t -> p c t", p=128), in_=xo[so][:]),
                    reads=[Bxo[so]], writes=[Bdram], sembuf=Bxo[so])
            for l in range(1, NL):
                cast_weights(l)
            S.flush()

        def rms_rstd(src_fn, N, sq, Bsq, rstd, Brstd, Bsrc, bank):
            for c in range(8):
                S.op("act", lambda e, c=c: e.activation(out=sq[:, c, 0:N], in_=src_fn(c), func=AF.Square),
                     reads=[Bsrc], writes=[Bsq])

            def mm(e):
                ins = None
                for c in range(8):
                    ins = e.matmul(psum[bank][:, 0:N], lhsT=onesb[:], rhs=sq[:, c, 0:N], start=(c == 0), stop=(c == 7))
                return ins
            S.op("pe", mm, reads=[Bsq, Bconst], writes=[Bps[bank]])
            S.op("act", lambda e: e.activation(out=rstd[:, 0:N], in_=psum[bank][:, 0:N], func=AF.Sqrt,
                                               bias=epsb[:, 0:1], scale=1.0 / D),
                 reads=[Bps[bank], Bconst], writes=[Brstd])
            S.op("dve", lambda e: e.reciprocal(out=rstd[:, 0:N], in_=rstd[:, 0:N]), reads=[Brstd], writes=[Brstd])

        def normalize(src_fn, gname, l, N, rstd, Brstd, Bsrc, dst, Bdst):
            for c in range(8):
                S.op("dve", lambda e, c=c: e.scalar_tensor_tensor(out=dst[:, c, 0:N], in0=src_fn(c), scalar=gcol(gname, l, c),
                                                                 in1=rstd[:, 0:N], op0=ALU.mult, op1=ALU.mult),
                     reads=[Bsrc, Brstd, Bconst], writes=[Bdst])

        def post_norm_residual(rbuf, Br, gname, l, xt, Bxt, sq, Bsq, rstd, Brstd, tmp, Btmp, bank):
            rms_rstd(lambda c: rbuf[:, c, :], 512, sq, Bsq, rstd, Brstd, Br, bank)
            for c in range(8):
                ti = c % 2
                S.op("dve", lambda e, c=c, ti=ti: e.scalar_tensor_tensor(
                    out=tmp[ti][:], in0=rbuf[:, c, :], scalar=gcol(gname, l, c), in1=rstd[:, 0:512],
                    op0=ALU.mult, op1=ALU.mult), reads=[Br, Brstd, Bconst], writes=[Btmp[ti]])
                S.op("pool", lambda e, c=c, ti=ti: e.tensor_tensor(out=xt[:, c, :], in0=xt[:, c, :], in1=tmp[ti][:],
                                                                  op=ALU.add), reads=[Btmp[ti], Bxt], writes=[Bxt])

        epsb = sb(es, "epsb", [128, 1], F32)
        S.op("dve", lambda e: e.memset(epsb[:], EPS), writes=[Bconst])

        def xT_blk(T):
            return xT[:, T * 512:(T + 1) * 512].rearrange("(c p) t -> p c t", p=128)

        for l in range(NL):
            with contextlib.ExitStack() as st:
                w_in_t = sb(st, "w_in_t", [128, 8, IN_W], BF16)
                wpb_t = sb(st, "wpb_t", [128, 4, D], BF16)
                pw_t = sb(st, "pw_t", [128, 4, 128], BF16)
                Bw1 = Buf("w1")
                load_w(w_in_t, "w_in", l, Bw1, 8)
                load_w(wpb_t, "w_pool_br", l, Bw1, 4)
                load_w(pw_t, "pool_w", l, Bw1, 4)
                xt = sb(st, "xt", [128, 8, 512], F32); Bxt = Buf("xt")
                sq = sb(st, "sq", [128, 8, 512], BF16); Bsq = Buf("sq")
                rstd = sb(st, "rstd", [128, 512], F32); Brstd = Buf("rstd")
                hT = [sb(st, "hT%d" % i, [128, 8, 512], BF16) for i in range(2)]; BhT = [Buf("hT0"), Buf("hT1")]
                qkst = sb(st, "qkst", [128, 8, 512], BF16); Bqk = Buf("qkst")
                vst = sb(st, "vst", [128, NH, 4, 64], BF16); Bvst = Buf("vst")
                ubuf = sb(st, "ubuf", [128, 4, 528], F32); Bu = Buf("ubuf")
                tA = sb(st, "tA", [128, 528], F32); BtA = Buf("tA")
                tB = sb(st, "tB", [128, 528], F32); BtB = Buf("tB")
                pooled = sb(st, "pooled", [128, 4, 512], BF16); Bpooled = Buf("pooled")
                mixed = sb(st, "mixed", [128, 4, 512], BF16); Bmixed = Buf("mixed")
                Ast = sb(st, "Ast", [128, 8, 512], BF16); BAst = Buf("Ast")
                Gfst = sb(st, "Gfst", [128, 8, 512], BF16); BGfst = Buf("Gfst")
                gtmp = [sb(st, "gtmp%d" % i, [128, 512], F32) for i in range(2)]; Bgtmp = [Buf("gt0"), Buf("gt1")]
                ncb = [sb(st, "ncb%d" % i, [8, 512], F32) for i in range(2)]; Bncb = [Buf("ncb0"), Buf("ncb1")]
                lf = sb(st, "lf", [8, 512], F32); Blf = Buf("lf")
                dq = sb(st, "dq", [8, 512], F32); Bdq = Buf("dq")
                dhl = sb(st, "dhl", [8, 2, 512], BF16); Bdhl = Buf("dhl")
                rot = Rot(range(1, 8))
                S.op("dve", lambda e: e.memset(ubuf[:], 0.0), writes=[Bu])
                for T in range(NB):
                    hs = T % 2
                    h = hT[hs]
                    S.dma("sp", lambda e, T=T: e.dma_start(out=xt[:], in_=xT_blk(T)), reads=[Bdram], writes=[Bxt],
                          sembuf=Bxt)
                    rms_rstd(lambda c: xt[:, c, :], 512, sq, Bsq, rstd, Brstd, Bxt, 0)
                    normalize(lambda c: xt[:, c, :], "mix_pre_g", l, 512, rstd, Brstd, Bxt, h, BhT[hs])

                    def proj(col0, M=128, bank=None, h=h, hs=hs):
                        b = rot.next() if bank is None else bank

                        def mm(e):
                            ins = None
                            for k in range(8):
                                ins = e.matmul(psum[b][0:M, :], lhsT=w_in_t[:, k, col0:col0 + M], rhs=h[:, k, :],
                                               start=(k == 0), stop=(k == 7))
                            return ins
                        S.op("pe", mm, reads=[BhT[hs], Bw1], writes=[Bps[b]])
                        return b

                    if T > 0:
                        S.op("pool", lambda e: e.tensor_copy(out=ubuf[:, :, 0:16], in_=ubuf[:, :, 512:528]),
                             reads=[Bu], writes=[Bu])
                    for g in range(4):
                        b = proj(g * 128)
                        S.op("act", lambda e, g=g, b=b: e.activation(out=ubuf[:, g, 16:528], in_=psum[b][:], func=AF.Copy),
                             reads=[Bps[b]], writes=[Bu])
                    for j in range(8):
                        b = proj(OFF_Q + j * 128)
                        if j % 2 == 0:
                            S.op("act", lambda e, j=j, b=b: e.activation(out=qkst[:, j, :], in_=psum[b][:], func=AF.Copy),
                                 reads=[Bps[b]], writes=[Bqk])
                        else:
                            S.op("dve", lambda e, j=j, b=b: e.tensor_copy(out=qkst[:, j, :], in_=psum[b][:]),
                                 reads=[Bps[b]], writes=[Bqk])
                    S.dma("sp", lambda e, T=T: e.dma_start(
                        out=QT[:, T * 512:(T + 1) * 512].rearrange("(j p) t -> p j t", p=128), in_=qkst[:, 0:4, :]),
                        reads=[Bqk], writes=[Bdram], sembuf=Bqk)
                    S.dma("sp", lambda e, T=T: e.dma_start(
                        out=KT[:, T * 512:(T + 1) * 512].rearrange("(j p) t -> p j t", p=128), in_=qkst[:, 4:8, :]),
                        reads=[Bqk], writes=[Bdram], sembuf=Bqk)
                    for i4 in range(4):
                        b = rot.next()

                        def mmv(e, b=b, i4=i4, h=h):
                            ins = None
                            for k in range(8):
                                ins = e.matmul(psum[b][:], lhsT=h[:, k, i4 * 128:(i4 + 1) * 128],
                                               rhs=w_in_t[:, k, OFF_V:OFF_V + 512], start=(k == 0), stop=(k == 7))
                            return ins
                        S.op("pe", mmv, reads=[BhT[hs], Bw1], writes=[Bps[b]])
                        S.op("dve", lambda e, b=b, i4=i4: e.tensor_copy(
                            out=vst[:, :, i4, :], in_=psum[b][:].rearrange("p (h d) -> p h d", h=NH)),
                            reads=[Bps[b]], writes=[Bvst])
                    S.dma("sp", lambda e, T=T: e.dma_start(out=Vs[:, :, T * 4:(T + 1) * 4, :], in_=vst[:]),
                          reads=[Bvst], writes=[Bdram], sembuf=Bvst)
                    b = proj(OFF_F, M=8)
                    S.op("act", lambda e, b=b: e.activation(out=lf[:], in_=psum[b][0:8, :], func=AF.Exp,
                                                            bias=negbf[:, l:l + 1], scale=-1.0),
                         reads=[Bps[b], Bconst], writes=[Blf])
                    S.op("act", lambda e: e.activation(out=lf[:], in_=lf[:], func=AF.Ln, bias=ones8[:, 0:1], scale=1.0),
                         reads=[Blf, Bconst], writes=[Blf])
                    cs = T % 2
                    if T == 0:
                        S.op("dve", lambda e, cs=cs: e.tensor_tensor_scan(out=ncb[cs][:], data0=ones8[:], data1=lf[:],
                                                                         initial=0.0, op0=ALU.mult, op1=ALU.add),
                             reads=[Blf, Bconst], writes=[Bncb[cs]])
                    else:
                        S.op("dve", lambda e, cs=cs: e.tensor_tensor_scan(out=ncb[cs][:], data0=ones8[:], data1=lf[:],
                                                                         initial=ncb[1 - cs][:, 511:512],
                                                                         op0=ALU.mult, op1=ALU.add),
                             reads=[Blf, Bconst, Bncb[1 - cs]], writes=[Bncb[cs]])
                    S.dma("sp", lambda e, T=T, cs=cs: e.dma_start(out=Cs[:, T * 512:(T + 1) * 512], in_=ncb[cs][:]),
                          reads=[Bncb[cs]], writes=[Bdram], sembuf=Bncb[cs])
                    S.op("dve", lambda e, cs=cs: e.tensor_scalar(out=dq[:], in0=ncb[cs][:], scalar1=ncb[cs][:, 0:1],
                                                                 scalar2=-8.0, op0=ALU.subtract, op1=ALU.mult),
                         reads=[Bncb[cs]], writes=[Bdq])
                    S.op("dve", lambda e: e.tensor_copy(out=dhl[:, 0, :], in_=dq[:]), reads=[Bdq], writes=[Bdhl])
                    S.op("dve", lambda e: e.tensor_tensor(out=dhl[:, 1, :], in0=dq[:], in1=dhl[:, 0, :], op=ALU.subtract),
                         reads=[Bdq, Bdhl], writes=[Bdhl])
                    S.dma("sp", lambda e, T=T: e.dma_start(out=CT[:, :, T * 512:(T + 1) * 512], in_=dhl[:]),
                          reads=[Bdhl], writes=[Bdram], sembuf=Bdhl)
                    for g in range(4):
                        w = 2 << g
                        src = ubuf[:, g, :]
                        srcB = Bu
                        sh = 1
                        lo = 1
                        pp = [(tA, BtA), (tB, BtB)]
                        for lev in range(g + 1):
                            dst, Bd = pp[lev % 2]
                            eng = "dve" if (g + lev) % 2 == 0 else "pool"
                            S.op(eng, lambda e, dst=dst, src=src, lo=lo, sh=sh: e.tensor_tensor(
                                out=dst[:, lo:528], in0=src[:, lo:528], in1=src[:, lo - sh:528 - sh], op=ALU.add),
                                reads=[srcB], writes=[Bd])
                            src, srcB = dst, Bd
                            sh *= 2
                            lo += sh
                        S.op("dve", lambda e, g=g, src=src, w=w: e.scalar_tensor_tensor(
                            out=pooled[:, g, :], in0=src[:, 16:528], scalar=1.0 / w, in1=ubuf[:, g, 16:528],
                            op0=ALU.mult, op1=ALU.subtract), reads=[srcB, Bu], writes=[Bpooled])
                        if T == 0:
                            other, Bo = pp[(g + 1) % 2]
                            S.op("dve", lambda e, g=g, src=src, other=other: e.tensor_tensor(
                                out=other[:, 0:16], in0=src[:, 16:32], in1=invc[:, g * 16:(g + 1) * 16], op=ALU.mult),
                                reads=[srcB, Bconst], writes=[Bo])
                            S.op("dve", lambda e, g=g, other=other: e.tensor_tensor(
                                out=pooled[:, g, 0:16], in0=other[:, 0:16], in1=ubuf[:, g, 16:32], op=ALU.subtract),
                                reads=[Bo, Bu], writes=[Bpooled])
                    for g in range(4):
                        b = rot.next()
                        S.op("pe", lambda e, g=g, b=b: e.matmul(psum[b][:], lhsT=pw_t[:, g, :], rhs=pooled[:, g, :],
                                                              start=True, stop=True),
                             reads=[Bpooled, Bw1], writes=[Bps[b]])
                        S.op("act", lambda e, g=g, b=b: e.activation(out=mixed[:, g, :], in_=psum[b][:], func=AF.Identity,
                                                                     scale=pscale[:, l * 4 + g:l * 4 + g + 1]),
                             reads=[Bps[b], Bconst], writes=[Bmixed])
                    for m in range(8):
                        by = rot.next()

                        def mmy(e, by=by, m=m):
                            ins = None
                            for g in range(4):
                                ins = e.matmul(psum[by][:], lhsT=wpb_t[:, g, m * 128:(m + 1) * 128], rhs=mixed[:, g, :],
                                               start=(g == 0), stop=(g == 3))
                            return ins
                        S.op("pe", mmy, reads=[Bmixed, Bw1], writes=[Bps[by]])
                        bg = proj(OFF_GP + m * 128)
                        ti = m % 2
                        S.op("act", lambda e, bg=bg, ti=ti: e.activation(out=gtmp[ti][:], in_=psum[bg][:], func=AF.Sigmoid),
                             reads=[Bps[bg]], writes=[Bgtmp[ti]])
                        S.op("dve", lambda e, by=by, ti=ti, m=m: e.tensor_tensor(out=Ast[:, m, :], in0=psum[by][:],
                                                                                in1=gtmp[ti][:], op=ALU.mult),
                             reads=[Bps[by], Bgtmp[ti]], writes=[BAst])
                        bf = proj(OFF_GF + m * 128)
                        S.op("act", lambda e, bf=bf, m=m: e.activation(out=Gfst[:, m, :], in_=psum[bf][:], func=AF.Sigmoid),
                             reads=[Bps[bf]], writes=[BGfst])
                    S.dma("sp", lambda e, T=T: e.dma_start(
                        out=Asc[:, T * 512:(T + 1) * 512].rearrange("(c p) t -> p c t", p=128), in_=Ast[:]),
                        reads=[BAst], writes=[Bdram], sembuf=BAst)
                    S.dma("sp", lambda e, T=T: e.dma_start(
                        out=Gfsc[:, T * 512:(T + 1) * 512].rearrange("(c p) t -> p c t", p=128), in_=Gfst[:]),
                        reads=[BGfst], writes=[Bdram], sembuf=BGfst)
                S.flush()

            with contextlib.ExitStack() as st:
                KA = [sb(st, "KA%d" % i, [66, SEQ], BF16) for i in range(2)]
                QA = [sb(st, "QA%d" % i, [66, SEQ], BF16) for i in range(2)]
                VA = [sb(st, "VA%d" % i, [128, NT, 128], BF16) for i in range(2)]
                nct = [sb(st, "nct%d" % i, [128, NT], F32) for i in range(2)]
                BH = [Buf("head0"), Buf("head1")]
                ncr = sb(st, "ncr", [128, NB], F32); Bncr = Buf("ncr")
                bias = sb(st, "bias", [128, NB, NT], F32); Bbias = Buf("bias")
                PT = [sb(st, "PT%d" % i, [128, 512], BF16) for i in range(3)]
                BPT = [Buf("PT%d" % i) for i in range(3)]
                rden = sb(st, "rden", [64, 512], F32); Brden = Buf("rden")
                cst = sb(st, "cst", [NT, 128], F32); Bcst = Buf("cst")
                ost = [sb(st, "ost%d" % i, [64, 512], BF16) for i in range(2)]
                Bost = [Buf("ost0"), Buf("ost1")]
                for i in range(2):
                    S.op("dve", lambda e, i=i: e.memset(KA[i][64:66, :], 1.0), writes=[BH[i]])
                    S.op("pool", lambda e, i=i: e.memset(VA[i][:, :, 64:128], 1.0), writes=[BH[i]])
                rotS = Rot([0, 1, 2])
                rotO = Rot([3, 4])
                rotP = Rot([0, 1, 2])
                for hd in range(NH):
                    s_ = hd % 2
                    S.dma("sp", lambda e, hd=hd, s_=s_: e.dma_start(out=KA[s_][0:64, :], in_=KT[hd * 64:(hd + 1) * 64, :]),
                          reads=[Bdram], writes=[BH[s_]], sembuf=BH[s_])
                    S.dma("sp", lambda e, hd=hd, s_=s_: e.dma_start(out=QA[s_][0:64, :], in_=QT[hd * 64:(hd + 1) * 64, :]),
                          reads=[Bdram], writes=[BH[s_]], sembuf=BH[s_])
                    S.dma("sp", lambda e, hd=hd, s_=s_: e.dma_start(out=QA[s_][64:66, :], in_=CT[hd]),
                          reads=[Bdram], writes=[BH[s_]], sembuf=BH[s_])
                    vq = max(1, NT // 4)
                    for v0 in range(0, NT, vq):
                        S.dma("sp", lambda e, hd=hd, s_=s_, v0=v0: e.dma_start(out=VA[s_][:, v0:v0 + vq, 0:64],
                                                                              in_=Vs[:, hd, v0:v0 + vq, :]),
                              reads=[Bdram], writes=[BH[s_]], sembuf=BH[s_])
                    S.dma("sp", lambda e, hd=hd: e.dma_start(out=cst[:], in_=Cs[hd].rearrange("(i p) -> i p", p=128)),
                          reads=[Bdram], writes=[Bcst], sembuf=Bcst)
                    S.op("pe", lambda e: e.transpose(psum[6][:, 0:NT], cst[:], identf[0:NT, 0:NT]),
                         reads=[Bcst, Bconst], writes=[Bps[6]])
                    S.op("dve", lambda e, s_=s_: e.tensor_copy(out=nct[s_][:], in_=psum[6][:, 0:NT]),
                         reads=[Bps[6]], writes=[BH[s_]])
                    S.op("pe", lambda e, s_=s_: e.matmul(psum[5][:, 0:NB], lhsT=e0[:],
                                                         rhs=nct[s_][:].rearrange("p (J a) -> p J a", a=4)[:, :, 0],
                                                         start=True, stop=True),
                         reads=[BH[s_], Bconst], writes=[Bps[5]])
                    S.op("dve", lambda e: e.tensor_copy(out=ncr[:], in_=psum[5][:, 0:NB]), reads=[Bps[5]], writes=[Bncr])
                    for J in range(NB):
                        S.op("dve", lambda e, J=J, s_=s_: e.tensor_scalar(out=bias[:, J, :], in0=nct[s_][:],
                                                                          scalar1=ncr[:, J:J + 1], scalar2=None,
                                                                          op0=ALU.subtract),
                             reads=[BH[s_], Bncr], writes=[Bbias])
                    for J in range(NB):
                        nk = 4 * J + 4
                        bo = rotO.next()
                        pend = None

                        def score(i, J=J, s_=s_):
                            a = i - 4 * J
                            off = 128 * a if a > 0 else 0
                            N = 512 - off
                            q0 = 512 * J + off
                            bs = rotS.next()

                            def mm(e):
                                ins = e.matmul(psum[bs][:, 0:N], lhsT=KA[s_][0:66, i * 128:(i + 1) * 128],
                                               rhs=QA[s_][0:66, q0:q0 + N], start=True, stop=(a < 0))
                                if a >= 0:
                                    ins = e.matmul(psum[bs][:, 0:128], lhsT=identb[:], rhs=maskneg[:], start=False, stop=True)
                                return ins
                            S.op("pe", mm, reads=[BH[s_], Bconst], writes=[Bps[bs]])
                            ps_ = rotP.next()
                            S.op("act", lambda e: e.activation(out=PT[ps_][:, 0:N], in_=psum[bs][:, 0:N], func=AF.Exp,
                                                               bias=bias[:, J, i:i + 1], scale=0.125),
                                 reads=[Bps[bs], Bbias], writes=[BPT[ps_]])
                            return (i, off, N, ps_)

                        def pv(item, J=J, s_=s_, bo=bo, nk=nk):
                            i, off, N, ps_ = item
                            S.op("pe", lambda e: e.matmul(psum[bo][:, off:off + N], lhsT=VA[s_][:, i, :], rhs=PT[ps_][:, 0:N],
                                                          start=(i == 0), stop=(i == nk - 1)),
                                 reads=[BH[s_], BPT[ps_]], writes=[Bps[bo]])

                        for i in range(nk):
                            item = score(i)
                            if pend is not None:
                                pv(pend)
                            pend = item
                        pv(pend)
                        os_ = (hd * NB + J) % 2
                        S.op("dve", lambda e, bo=bo: e.reciprocal(out=rden[:], in_=psum[bo][64:128, :]),
                             reads=[Bps[bo]], writes=[Brden])
                        S.op("dve", lambda e, bo=bo, os_=os_: e.tensor_tensor(out=ost[os_][:], in0=psum[bo][0:64, :],
                                                                             in1=rden[:], op=ALU.mult),
                             reads=[Bps[bo], Brden], writes=[Bost[os_]])
                        S.dma("sp", lambda e, hd=hd, J=J, os_=os_: e.dma_start(
                            out=OT[hd * 64:(hd + 1) * 64, J * 512:(J + 1) * 512], in_=ost[os_][:]),
                            reads=[Bost[os_]], writes=[Bdram], sembuf=Bost[os_])
                S.flush()

            with contextlib.ExitStack() as st:
                wfb = sb(st, "wfb", [128, 4, D], BF16)
                wmo = sb(st, "wmo", [128, 8, D], BF16)
                wxq = sb(st, "wxq", [128, 8, 512], BF16)
                wxo = sb(st, "wxo", [128, 4, D], BF16)
                wkv = sb(st, "wkv", [128, 8, 1024], BF16)
                Bw3 = Buf("w3")
                load_w(wkv, "w_xkv", l, Bw3, 8)
                load_w(wfb, "w_fox_br", l, Bw3, 4)
                load_w(wmo, "w_mix_out", l, Bw3, 8)
                load_w(wxq, "w_xq", l, Bw3, 8)
                load_w(wxo, "w_xo", l, Bw3, 4)
                xt = sb(st, "xt3", [128, 8, 512], F32); Bxt = Buf("xt3")
                sq = sb(st, "sq3", [128, 8, 512], BF16); Bsq = Buf("sq3")
                rstd = sb(st, "rstd3", [128, 512], F32); Brstd = Buf("rstd3")
                Ain = sb(st, "Ain", [128, 8, 512], BF16); BAin = Buf("Ain")
                Gin = sb(st, "Gin", [128, 8, 512], BF16); BGin = Buf("Gin")
                Oin = sb(st, "Oin", [128, 4, 512], BF16); BOin = Buf("Oin")
                merged = sb(st, "merged", [128, 8, 512], BF16); Bmerged = Buf("merged")
                rbuf = sb(st, "rbuf", [128, 8, 512], F32); Br = Buf("rbuf")
                tmp = [sb(st, "tmp3%d" % i, [128, 512], F32) for i in range(2)]; Btmp = [Buf("tmp30"), Buf("tmp31")]
                h2 = sb(st, "h2", [128, 8, 512], BF16); Bh2 = Buf("h2")
                qx = sb(st, "qx", [128, 4, 512], BF16); Bqx = Buf("qx")
                PTx = [sb(st, "PTx%d" % i, [128, 2, 512], BF16) for i in range(2)]; BPTx = [Buf("PTx0"), Buf("PTx1")]
                ox = sb(st, "ox", [128, 4, 512], BF16); Box = Buf("ox")
                kmT = sb(st, "kmT", [128, 4, NMEM], BF16); vm = sb(st, "vm", [128, 2, 512], BF16)
                memn = sb(st, "memn", [128, 8, NMEM], BF16); Bmemn = Buf("memn"); Bkv = Buf("kv")
                rot = Rot(range(1, 8))
                rms_rstd(lambda c: memT[:, c, :], NMEM, sq, Bsq, rstd, Brstd, Bmem, 0)
                normalize(lambda c: memT[:, c, :], "mem_g", l, NMEM, rstd, Brstd, Bmem, memn, Bmemn)
                for hh in range(4):
                    b = rot.next()

                    def mmk(e, b=b, hh=hh):
                        ins = None
                        for k in range(8):
                            ins = e.matmul(psum[b][:, 0:NMEM], lhsT=wkv[:, k, hh * 128:(hh + 1) * 128], rhs=memn[:, k, :],
                                           start=(k == 0), stop=(k == 7))
                        return ins
                    S.op("pe", mmk, reads=[Bmemn, Bw3], writes=[Bps[b]])
                    S.op("act", lambda e, b=b, hh=hh: e.activation(out=kmT[:, hh, :], in_=psum[b][:, 0:NMEM], func=AF.Copy),
                         reads=[Bps[b]], writes=[Bkv])
                for mt in range(2):
                    b = rot.next()

                    def mmvm(e, b=b, mt=mt):
                        ins = None
                        for k in range(8):
                            ins = e.matmul(psum[b][:], lhsT=memn[:, k, mt * 128:(mt + 1) * 128], rhs=wkv[:, k, 512:1024],
                                           start=(k == 0), stop=(k == 7))
                        return ins
                    S.op("pe", mmvm, reads=[Bmemn, Bw3], writes=[Bps[b]])
                    S.op("act", lambda e, b=b, mt=mt: e.activation(out=vm[:, mt, :], in_=psum[b][:], func=AF.Copy),
                         reads=[Bps[b]], writes=[Bkv])
                for T in range(NB):
                    def ld(dst, src, Bd, T=T):
                        S.dma("sp", lambda e: e.dma_start(out=dst[:], in_=src[:, T * 512:(T + 1) * 512].rearrange(
                            "(c p) t -> p c t", p=128)), reads=[Bdram], writes=[Bd], sembuf=Bd)
                    ld(xt, xT, Bxt); ld(Ain, Asc, BAin); ld(Gin, Gfsc, BGin); ld(Oin, OT, BOin)
                    for m in range(8):
                        b = rot.next()

                        def mmf(e, b=b, m=m):
                            ins = None
                            for k in range(4):
                                ins = e.matmul(psum[b][:], lhsT=wfb[:, k, m * 128:(m + 1) * 128], rhs=Oin[:, k, :],
                                               start=(k == 0), stop=(k == 3))
                            return ins
                        S.op("pe", mmf, reads=[BOin, Bw3], writes=[Bps[b]])
                        ti = m % 2
                        S.op("dve", lambda e, b=b, m=m, ti=ti: e.tensor_tensor(out=tmp[ti][:], in0=psum[b][:], in1=Gin[:, m, :],
                                                                              op=ALU.mult),
                             reads=[Bps[b], BGin], writes=[Btmp[ti]])
                        S.op("pool", lambda e, m=m, ti=ti: e.tensor_tensor(out=merged[:, m, :], in0=tmp[ti][:], in1=Ain[:, m, :],
                                                                          op=ALU.add),
                             reads=[Btmp[ti], BAin], writes=[Bmerged])
                    for m in range(8):
                        b = rot.next()

                        def mmo(e, b=b, m=m):
                            ins = None
                            for k in range(8):
                                ins = e.matmul(psum[b][:], lhsT=wmo[:, k, m * 128:(m + 1) * 128], rhs=merged[:, k, :],
                                               start=(k == 0), stop=(k == 7))
                            return ins
                        S.op("pe", mmo, reads=[Bmerged, Bw3], writes=[Bps[b]])
                        S.op("act", lambda e, b=b, m=m: e.activation(out=rbuf[:, m, :], in_=psum[b][:], func=AF.Copy),
                             reads=[Bps[b]], writes=[Br])
                    post_norm_residual(rbuf, Br, "mix_post_g", l, xt, Bxt, sq, Bsq, rstd, Brstd, tmp, Btmp, 0)
                    rms_rstd(lambda c: xt[:, c, :], 512, sq, Bsq, rstd, Brstd, Bxt, 0)
                    normalize(lambda c: xt[:, c, :], "xa_pre_g", l, 512, rstd, Brstd, Bxt, h2, Bh2)
                    for hh in range(4):
                        b = rot.next()

                        def mmq(e, b=b, hh=hh):
                            ins = None
                            for k in range(8):
                                ins = e.matmul(psum[b][:], lhsT=wxq[:, k, hh * 128:(hh + 1) * 128], rhs=h2[:, k, :],
                                               start=(k == 0), stop=(k == 7))
                            return ins
                        S.op("pe", mmq, reads=[Bh2, Bw3], writes=[Bps[b]])
                        S.op("act", lambda e, b=b, hh=hh: e.activation(out=qx[:, hh, :], in_=psum[b][:], func=AF.Copy),
                             reads=[Bps[b]], writes=[Bqx])
                    for hh in range(4):
                        px = hh % 2
                        for mt in range(2):
                            b = rot.next()
                            S.op("pe", lambda e, b=b, hh=hh, mt=mt: e.matmul(
                                psum[b][:], lhsT=kmT[:, hh, mt * 128:(mt + 1) * 128], rhs=qx[:, hh, :], start=True, stop=True),
                                reads=[Bkv, Bqx], writes=[Bps[b]])
                            S.op("act", lambda e, b=b, px=px, mt=mt: e.activation(
                                out=PTx[px][:, mt, :], in_=psum[b][:], func=AF.Exp, scale=1.0 / math.sqrt(128.0)),
                                reads=[Bps[b]], writes=[BPTx[px]])
                        bo = rot.next()
                        bd = rot.next()

                        def mmpv(e, bo=bo, hh=hh, px=px):
                            ins = None
                            for mt in range(2):
                                ins = e.matmul(psum[bo][:], lhsT=vm[:, mt, hh * 128:(hh + 1) * 128], rhs=PTx[px][:, mt, :],
                                               start=(mt == 0), stop=(mt == 1))
                            return ins
                        S.op("pe", mmpv, reads=[Bkv, BPTx[px]], writes=[Bps[bo]])

                        def mmden(e, bd=bd, px=px):
                            ins = None
                            for mt in range(2):
                                ins = e.matmul(psum[bd][:], lhsT=onesb[:], rhs=PTx[px][:, mt, :], start=(mt == 0), stop=(mt == 1))
                            return ins
                        S.op("pe", mmden, reads=[Bconst, BPTx[px]], writes=[Bps[bd]])
                        ti = hh % 2
                        S.op("dve", lambda e, bd=bd, ti=ti: e.reciprocal(out=tmp[ti][:], in_=psum[bd][:]),
                             reads=[Bps[bd]], writes=[Btmp[ti]])
                        S.op("dve", lambda e, bo=bo, ti=ti, hh=hh: e.tensor_tensor(out=ox[:, hh, :], in0=psum[bo][:],
                                                                                  in1=tmp[ti][:], op=ALU.mult),
                             reads=[Bps[bo], Btmp[ti]], writes=[Box])
                    for m in range(8):
                        b = rot.next()

                        def mmxo(e, b=b, m=m):
                            ins = None
                            for k in range(4):
                                ins = e.matmul(psum[b][:], lhsT=wxo[:, k, m * 128:(m + 1) * 128], rhs=ox[:, k, :],
                                               start=(k == 0), stop=(k == 3))
                            return ins
                        S.op("pe", mmxo, reads=[Box, Bw3], writes=[Bps[b]])
                        S.op("act", lambda e, b=b, m=m: e.activation(out=rbuf[:, m, :], in_=psum[b][:], func=AF.Copy),
                             reads=[Bps[b]], writes=[Br])
                    post_norm_residual(rbuf, Br, "xa_post_g", l, xt, Bxt, sq, Bsq, rstd, Brstd, tmp, Btmp, 0)
                    S.dma("sp", lambda e, T=T: e.dma_start(out=xT_blk(T), in_=xt[:]), reads=[Bxt], writes=[Bdram],
                          sembuf=Bxt)
                S.flush()

            with contextlib.ExitStack() as st:
                wup = sb(st, "wup", [128, 8, 2 * DFF], BF16); Bw4 = Buf("w4")
                load_w(wup, "w_up", l, Bw4, 8)
                xt = sb(st, "xt4", [128, 8, 512], F32); Bxt = Buf("xt4")
                sq = sb(st, "sq4", [128, 8, 512], BF16); Bsq = Buf("sq4")
                rstd = sb(st, "rstd4", [128, 512], F32); Brstd = Buf("rstd4")
                h3 = [sb(st, "h3%d" % i, [128, 8, 512], BF16) for i in range(2)]; Bh3 = [Buf("h30"), Buf("h31")]
                zb = [sb(st, "zb%d" % i, [128, 514], F32) for i in range(4)]; Bzb = [Buf("zb%d" % i) for i in range(4)]
                acc = [sb(st, "acc%d" % i, [128, 512], F32) for i in range(4)]; Bacc = [Buf("acc%d" % i) for i in range(4)]
                halo = sb(st, "halo", [128, 44, 2], F32); Bhalo = Buf("halo")
                gl = [sb(st, "gl%d" % i, [128, 512], F32) for i in range(2)]; Bgl = [Buf("gl0"), Buf("gl1")]
                gu = [sb(st, "gu0", [128, 22, 512], BF16)] * 2; Bgu = [Buf("gu0")] * 2
                rot = Rot(range(1, 8))
                S.op("dve", lambda e: e.memset(halo[:], 0.0), writes=[Bhalo])
                zi = 0
                for T in range(NB):
                    hs = T % 2
                    S.dma("sp", lambda e, T=T: e.dma_start(out=xt[:], in_=xT_blk(T)), reads=[Bdram], writes=[Bxt], sembuf=Bxt)
                    rms_rstd(lambda c: xt[:, c, :], 512, sq, Bsq, rstd, Brstd, Bxt, 0)
                    normalize(lambda c: xt[:, c, :], "ffn_pre_g", l, 512, rstd, Brstd, Bxt, h3[hs], Bh3[hs])
                    for j in range(22):
                        accs = []
                        for half in range(2):
                            cj = j + 22 * half
                            b = rot.next()

                            def mmu(e, b=b, cj=cj, hs=hs):
                                ins = None
                                for k in range(8):
                                    ins = e.matmul(psum[b][:], lhsT=wup[:, k, cj * 128:(cj + 1) * 128], rhs=h3[hs][:, k, :],
                                                   start=(k == 0), stop=(k == 7))
                                return ins
                            S.op("pe", mmu, reads=[Bh3[hs], Bw4], writes=[Bps[b]])
                            z = zi % 4
                            zi += 1
                            cw = lambda jj, cj=cj: convw[:, (l * 3 + jj) * 44 + cj:(l * 3 + jj) * 44 + cj + 1]
                            S.op("pool", lambda e, z=z, cj=cj: e.tensor_copy(out=zb[z][:, 0:2], in_=halo[:, cj, :]),
                                 reads=[Bhalo], writes=[Bzb[z]])
                            S.op("act", lambda e, z=z, b=b: e.activation(out=zb[z][:, 2:514], in_=psum[b][:], func=AF.Copy),
                                 reads=[Bps[b]], writes=[Bzb[z]])
                            S.op("act", lambda e, z=z, b=b, cj=cj, cw=cw: e.activation(
                                out=acc[z][:], in_=psum[b][:], func=AF.Identity, scale=cw(2),
                                bias=convb[:, l * 44 + cj:l * 44 + cj + 1]),
                                reads=[Bps[b], Bconst], writes=[Bacc[z]])
                            S.op("pool", lambda e, z=z, cj=cj: e.tensor_copy(out=halo[:, cj, :], in_=zb[z][:, 512:514]),
                                 reads=[Bzb[z]], writes=[Bhalo])
                            S.op("dve", lambda e, z=z, cw=cw: e.scalar_tensor_tensor(
                                out=acc[z][:], in0=zb[z][:, 1:513], scalar=cw(1), in1=acc[z][:], op0=ALU.mult, op1=ALU.add),
                                reads=[Bzb[z], Bacc[z], Bconst], writes=[Bacc[z]])
                            S.op("dve", lambda e, z=z, cw=cw: e.scalar_tensor_tensor(
                                out=acc[z][:], in0=zb[z][:, 0:512], scalar=cw(0), in1=acc[z][:], op0=ALU.mult, op1=ALU.add),
                                reads=[Bzb[z], Bacc[z], Bconst], writes=[Bacc[z]])
                            accs.append(z)
                        gi = j % 2
                        S.op("act", lambda e, gi=gi, za=accs[0]: e.activation(out=gl[gi][:], in_=acc[za][:],
                                                                              func=AF.Gelu_apprx_tanh),
                             reads=[Bacc[accs[0]]], writes=[Bgl[gi]])
                        S.op("pool", lambda e, gi=gi, zu=accs[1], j=j, hs=hs: e.tensor_tensor(
                            out=gu[hs][:, j, :], in0=gl[gi][:], in1=acc[zu][:], op=ALU.mult),
                            reads=[Bgl[gi], Bacc[accs[1]]], writes=[Bgu[hs]])
                    S.dma("sp", lambda e, T=T, hs=hs: e.dma_start(
                        out=GU[:, T * 512:(T + 1) * 512].rearrange("(c p) t -> p c t", p=128), in_=gu[hs][:]),
                        reads=[Bgu[hs]], writes=[Bdram], sembuf=Bgu[hs])
                S.flush()

            with contextlib.ExitStack() as st:
                wdn = sb(st, "wdn", [128, 22, D], BF16); Bw5 = Buf("w5")
                load_w(wdn, "w_down", l, Bw5, 22)
                xts = [sb(st, "xt5%d" % i, [128, 8, 512], F32) for i in range(2)]; Bxts = [Buf("xt50"), Buf("xt51")]
                gin = [sb(st, "gin%d" % i, [128, 22, 512], BF16) for i in range(2)]; Bgin = [Buf("gin0"), Buf("gin1")]
                sq = sb(st, "sq5", [128, 8, 512], BF16); Bsq = Buf("sq5")
                rstd = sb(st, "rstd5", [128, 512], F32); Brstd = Buf("rstd5")
                rbuf = sb(st, "rbuf5", [128, 8, 512], F32); Br = Buf("rbuf5")
                tmp = [sb(st, "tmp5%d" % i, [128, 512], F32) for i in range(2)]; Btmp = [Buf("tmp50"), Buf("tmp51")]
                last = (l == NL - 1)
                if last:
                    xo = [sb(st, "xo5%d" % i, [128, D], F32) for i in range(2)]; Bxo5 = [Buf("xo50"), Buf("xo51")]
                rot = Rot(range(1, 8))
                for T in range(NB):
                    s_ = T % 2
                    xt, Bxt = xts[s_], Bxts[s_]
                    S.dma("sp", lambda e, T=T, xt=xt: e.dma_start(out=xt[:], in_=xT_blk(T)), reads=[Bdram], writes=[Bxt],
                          sembuf=Bxt)
                    S.dma("sp", lambda e, T=T, s_=s_: e.dma_start(
                        out=gin[s_][:], in_=GU[:, T * 512:(T + 1) * 512].rearrange("(c p) t -> p c t", p=128)),
                        reads=[Bdram], writes=[Bgin[s_]], sembuf=Bgin[s_])
                    for m in range(8):
                        b = rot.next()

                        def mmd(e, b=b, m=m, s_=s_):
                            ins = None
                            for k in range(22):
                                ins = e.matmul(psum[b][:], lhsT=wdn[:, k, m * 128:(m + 1) * 128], rhs=gin[s_][:, k, :],
                                               start=(k == 0), stop=(k == 21))
                            return ins
                        S.op("pe", mmd, reads=[Bgin[s_], Bw5], writes=[Bps[b]])
                        S.op("act", lambda e, b=b, m=m: e.activation(out=rbuf[:, m, :], in_=psum[b][:], func=AF.Copy),
                             reads=[Bps[b]], writes=[Br])
                    post_norm_residual(rbuf, Br, "ffn_post_g", l, xt, Bxt, sq, Bsq, rstd, Brstd, tmp, Btmp, 0)
                    if not last:
                        S.dma("sp", lambda e, T=T, xt=xt: e.dma_start(out=xT_blk(T), in_=xt[:]), reads=[Bxt], writes=[Bdram],
                              sembuf=Bxt)
                    else:
                        for i4 in range(4):
                            tt = T * 4 + i4
                            so = tt % 2
                            for c in range(8):
                                b = rot.next()
                                S.op("pe", lambda e, c=c, b=b, i4=i4, xt=xt: e.transpose(
                                    psum[b][:, 0:128], xt[:, c, i4 * 128:(i4 + 1) * 128], identf[:]),
                                    reads=[Bxt, Bconst], writes=[Bps[b]])
                                if c % 2 == 0:
                                    S.op("dve", lambda e, c=c, b=b, so=so: e.tensor_copy(
                                        out=xo[so][:, c * 128:(c + 1) * 128], in_=psum[b][:, 0:128]),
                                        reads=[Bps[b]], writes=[Bxo5[so]])
                                else:
                                    S.op("act", lambda e, c=c, b=b, so=so: e.activation(
                                        out=xo[so][:, c * 128:(c + 1) * 128], in_=psum[b][:, 0:128], func=AF.Copy),
                                        reads=[Bps[b]], writes=[Bxo5[so]])
                            S.dma("sp", lambda e, tt=tt, so=so: e.dma_start(out=out_d[tt * 128:(tt + 1) * 128, :], in_=xo[so][:]),
                                  reads=[Bxo5[so]], writes=[Bdram], sembuf=Bxo5[so])
                S.flush()
    return nc, S


def host_consts():
    bf = ml_dtypes.bfloat16
    identf = np.eye(128, dtype=np.float32)
    e0 = np.zeros((128, 128), np.float32)
    e0[0, :] = 1.0
    invc = np.zeros((128, 64), np.float32)
    for g, w in enumerate((2, 4, 8, 16)):
        for t in range(16):
            invc[:, g * 16 + t] = 1.0 / min(t + 1, w)
    s = np.arange(128)[:, None]
    t = np.arange(128)[None, :]
    maskneg = np.where(t >= s, 0.0, NEG).astype(np.float32)
    return {"c_identf": identf, "c_e0": e0, "c_invc": invc, "c_identb": identf.astype(bf),
            "c_onesb": np.ones((128, 128), np.float32).astype(bf), "c_maskneg": maskneg.astype(bf)}


_WNAMES = ("mix_pre_g", "mix_post_g", "xa_pre_g", "xa_post_g", "mem_g", "ffn_pre_g", "ffn_post_g", "w_in", "b_forget",
           "pool_w", "pool_scale", "w_pool_br", "w_fox_br", "w_mix_out", "w_xq", "w_xkv", "w_xo", "w_up", "conv_w",
           "conv_b", "w_down")


def run(inputs, SEQ, NL, dbg=(), trace=False):
    nc, _S = build_program(SEQ, NL, dbg)
    consts = host_consts()
    B = inputs["x"].shape[0]
    shared = {n: np.ascontiguousarray(np.asarray(inputs[n], dtype=np.float32)[:NL]) for n in _WNAMES}
    shared.update(consts)
    in_maps = []
    for b in range(B):
        m = dict(shared)
        m["x"] = np.ascontiguousarray(np.asarray(inputs["x"][b], dtype=np.float32))
        m["mem"] = np.ascontiguousarray(np.asarray(inputs["mem"][b], dtype=np.float32))
        in_maps.append(m)
    res = run_bass_kernel_spmd(nc, in_maps, core_ids=list(range(B)), trace=trace)
    return res


def kernel(**inputs):
    res = run(inputs, 8192, 4)
    return np.stack([np.asarray(r["out"], dtype=np.float32) for r in res.results], axis=0)
```
